# Optimizing a Trainium2 kernel written in Bass

```python
import math
import jax, jax.numpy as jnp
from jax import lax
import numpy as np

D_MODEL = 4096
BATCH = 2
SEQ = 8192
DEPTH = 4
DEC_BATCH = 4
DEC_SEQ = 4096
PAST_LEN = 128

A_HEAD_DIM = 128
A_HEADS = D_MODEL // 256
A_WIDTH = A_HEADS * A_HEAD_DIM
CONV_K = 5
CHUNK = 64
DT_MIN = 1e-3
DT_MAX = 1e-1
B_GROUPS = ((128, 1), (512, 4), (2048, 16))
N_GROUPS = len(B_GROUPS)
B_HEAD_DIM = 128
B_HEADS = D_MODEL // 512
B_GROUP_WIDTH = B_HEADS * B_HEAD_DIM
B_QKV_WIDTH = N_GROUPS * B_GROUP_WIDTH
ROPE_THETA = 10000.0
NEG_INF = -1e30
EPS = 1e-6
L2_EPS = 1e-6
ADA_INIT = 0.5
SPLIT_SIZES = (A_WIDTH, A_WIDTH, A_WIDTH, A_WIDTH, 2 * A_HEADS, 2 * A_HEADS,
               B_QKV_WIDTH, B_QKV_WIDTH, B_QKV_WIDTH, B_GROUP_WIDTH, 2 * D_MODEL)
N_IN = sum(SPLIT_SIZES)
SPLIT_POINTS = tuple(int(s) for s in np.cumsum(SPLIT_SIZES)[:-1])

kernel_name = 'hybrid_gdn_dilated_window_encoder'


def _rms_norm(x, gain):
    xf = x.astype(jnp.float32)
    y = xf * lax.rsqrt(jnp.mean(xf * xf, axis=-1, keepdims=True) + EPS)
    return (y * gain.astype(jnp.float32)).astype(x.dtype)


def _l2norm(t):
    return t * lax.rsqrt(jnp.sum(t * t, axis=-1, keepdims=True) + L2_EPS)


def _centred_dwconv(x, w):
    pad = CONV_K // 2
    return lax.conv_general_dilated(
        x, w[:, None, :].astype(x.dtype), window_strides=(1,),
        padding=[(pad, pad)], dimension_numbers=('NWC', 'WIO', 'NWC'),
        feature_group_count=x.shape[-1])


def _rope(t, pos):
    half = t.shape[-1] // 2
    inv = ROPE_THETA ** (-jnp.arange(half, dtype=jnp.float32) / half)
    ang = pos.astype(jnp.float32)[:, None] * inv[None, :]
    cos = jnp.cos(ang)[None, :, None, :]
    sin = jnp.sin(ang)[None, :, None, :]
    t1, t2 = t[..., :half], t[..., half:]
    return jnp.concatenate([t1 * cos - t2 * sin, t1 * sin + t2 * cos], axis=-1)


def _chunk_gated_delta(q, k, v, g, beta):
    Bn, S, H, dk = q.shape
    dv = v.shape[-1]
    nc = S // CHUNK

    def to_chunks(t):
        return jnp.moveaxis(t, 2, 1).reshape((Bn, H, nc, CHUNK) + t.shape[3:])

    q, k, v, g, beta = (to_chunks(t) for t in (q, k, v, g, beta))
    g = jnp.cumsum(g, axis=-1)
    idx = jnp.arange(CHUNK)
    lower = idx[:, None] >= idx[None, :]
    strict = idx[:, None] > idx[None, :]
    decay = jnp.exp(jnp.where(lower, g[..., :, None] - g[..., None, :], -jnp.inf))
    kb = k * beta[..., None]
    L = jnp.where(strict, jnp.einsum('bhncd,bhnmd->bhncm', kb, k) * decay, 0.0)
    eye = jnp.eye(CHUNK, dtype=jnp.float32)
    T = lax.linalg.triangular_solve(eye + L, jnp.broadcast_to(eye, L.shape),
                                    left_side=True, lower=True)
    eg = jnp.exp(g)
    u = jnp.einsum('bhncm,bhnmd->bhncd', T, v * beta[..., None])
    w = jnp.einsum('bhncm,bhnmd->bhncd', T, kb * eg[..., None])
    attn = jnp.einsum('bhncd,bhnmd->bhncm', q, k) * decay
    q_dec = q * eg[..., None]
    k_dec = k * jnp.exp(g[..., -1:] - g)[..., None]
    g_last = eg[..., -1]

    def step(state, xs):
        qd, kd, uc, wc, ac, gl = xs
        v_new = uc - jnp.einsum('bhck,bhkv->bhcv', wc, state)
        o = jnp.einsum('bhck,bhkv->bhcv', qd, state) + jnp.einsum('bhcm,bhmv->bhcv', ac, v_new)
        state = state * gl[..., None, None] + jnp.einsum('bhck,bhcv->bhkv', kd, v_new)
        return state, o

    xs = tuple(jnp.moveaxis(t, 2, 0) for t in (q_dec, k_dec, u, w, attn, g_last))
    state0 = jnp.zeros((Bn, H, dk, dv), jnp.float32)
    _, o = lax.scan(step, state0, xs)
    o = jnp.moveaxis(o, 0, 2).reshape(Bn, H, S, dv)
    return jnp.moveaxis(o, 1, 2)


def _deltanet_branch(qa, ka, va, za, aa, ba, conv_w, a_log, dt_bias, norm_w):
    Bn, S, _ = qa.shape
    f32 = jnp.float32
    qkv = jax.nn.silu(_centred_dwconv(jnp.concatenate([qa, ka, va], axis=-1), conv_w))
    q, k, v = jnp.split(qkv.astype(f32), 3, axis=-1)
    shp = (Bn, S, A_HEADS, A_HEAD_DIM)
    q = _l2norm(q.reshape(shp)) * (A_HEAD_DIM ** -0.5)
    k = _l2norm(k.reshape(shp))
    v = v.reshape(shp)
    a = aa.astype(f32).reshape(Bn, S, 2, A_HEADS)
    b = ba.astype(f32).reshape(Bn, S, 2, A_HEADS)
    g = -jnp.exp(a_log.astype(f32)) * jax.nn.softplus(a + dt_bias.astype(f32))
    beta = jax.nn.sigmoid(b)
    o_fwd = _chunk_gated_delta(q, k, v, g[:, :, 0], beta[:, :, 0])
    rev = lambda t: jnp.flip(t, axis=1)
    o_bwd = rev(_chunk_gated_delta(rev(q), rev(k), rev(v), rev(g[:, :, 1]), rev(beta[:, :, 1])))
    o = _rms_norm(o_fwd + o_bwd, norm_w) * jax.nn.silu(za.astype(f32).reshape(shp))
    return o.reshape(Bn, S, A_WIDTH).astype(qa.dtype)


def _strided_window_attention(q, k, v, window, dilation):
    Bn, S, H, hd = q.shape
    R = window // (2 * dilation)
    L = S // dilation
    nb = -(-L // R)
    Lp = nb * R

    def residue_major(t):
        t = t.reshape(Bn, L, dilation, H, hd).transpose(0, 2, 3, 1, 4)
        return jnp.pad(t, ((0, 0), (0, 0), (0, 0), (0, Lp - L), (0, 0)))

    def neighbours(t):
        tp = jnp.pad(t, ((0, 0), (0, 0), (0, 0), (R, R), (0, 0)))
        tp = tp.reshape(Bn, dilation, H, nb + 2, R, hd)
        return jnp.concatenate([tp[:, :, :, :-2], tp[:, :, :, 1:-1], tp[:, :, :, 2:]], axis=4)

    qb = residue_major(q).reshape(Bn, dilation, H, nb, R, hd)
    kn = neighbours(residue_major(k))
    vn = neighbours(residue_major(v))
    qpos = jnp.arange(nb)[:, None] * R + jnp.arange(R)[None, :]
    kpos = (jnp.arange(nb)[:, None] - 1) * R + jnp.arange(3 * R)[None, :]
    rel = kpos[:, None, :] - qpos[:, :, None]
    valid = (jnp.abs(rel) <= R) & (kpos[:, None, :] >= 0) & (kpos[:, None, :] < L)
    s = jnp.einsum('bdhnqc,bdhnkc->bdhnqk', qb, kn) * (hd ** -0.5)
    s = jnp.where(valid, s, NEG_INF)
    m = jnp.max(s, axis=-1, keepdims=True)
    p = jnp.exp(s - m)
    den = jnp.sum(p, axis=-1)
    o = jnp.einsum('bdhnqk,bdhnkc->bdhnqc', p, vn) / den[..., None]
    lse = m[..., 0] + jnp.log(den)
    o = o.reshape(Bn, dilation, H, Lp, hd)[:, :, :, :L].transpose(0, 3, 1, 2, 4).reshape(Bn, S, H, hd)
    lse = lse.reshape(Bn, dilation, H, Lp)[..., :L].transpose(0, 3, 1, 2).reshape(Bn, S, H)
    return o, lse


def _dilated_branch(qb, kb, vb, zb):
    Bn, S, _ = qb.shape
    f32 = jnp.float32
    pos = jnp.arange(S)
    flat = (Bn, S, N_GROUPS * B_HEADS, B_HEAD_DIM)
    shp = (Bn, S, N_GROUPS, B_HEADS, B_HEAD_DIM)
    q = _rope(qb.astype(f32).reshape(flat), pos).reshape(shp)
    k = _rope(kb.astype(f32).reshape(flat), pos).reshape(shp)
    v = vb.astype(f32).reshape(shp)
    outs, lses = [], []
    for gi, (window, dilation) in enumerate(B_GROUPS):
        o, lse = _strided_window_attention(q[:, :, gi], k[:, :, gi], v[:, :, gi], window, dilation)
        outs.append(o)
        lses.append(lse)
    o = jnp.stack(outs, axis=2)
    wts = jax.nn.softmax(jnp.stack(lses, axis=2), axis=2)
    o = jnp.sum(wts[..., None] * o, axis=2)
    o = o * jax.nn.silu(zb.astype(f32).reshape(Bn, S, B_HEADS, B_HEAD_DIM))
    return o.reshape(Bn, S, B_GROUP_WIDTH).astype(qb.dtype)


def _encoder_layer(x, c, w_ada, b_ada, g_pre, g_post, w_in, conv_a, a_log, dt_bias,
                   norm_a, w_up_a, w_up_b, w_out):
    mod = jax.nn.silu(c) @ w_ada + b_ada
    shift, scale, gate = jnp.split(mod[:, None, :], 3, axis=-1)
    h = _rms_norm(x, g_pre) * (1 + scale) + shift
    proj = h @ w_in
    qa, ka, va, za, aa, ba, qb, kb, vb, zb, gates = jnp.split(proj, SPLIT_POINTS, axis=-1)
    ya = _deltanet_branch(qa, ka, va, za, aa, ba, conv_a, a_log, dt_bias, norm_a)
    yb = _dilated_branch(qb, kb, vb, zb)
    gate_a, gate_b = jnp.split(jax.nn.sigmoid(gates), 2, axis=-1)
    merged = gate_a * (ya @ w_up_a) + gate_b * (yb @ w_up_b)
    out = merged @ w_out
    return x + gate * _rms_norm(out, g_post)


def setup_inputs(seed: int = 0) -> dict:
    key = jax.random.key(seed)
    ks = jax.random.split(key, 16)
    f32 = jnp.float32

    def nrm(k, shape, std):
        return std * jax.random.normal(k, shape, f32)

    dt = jnp.exp(jax.random.uniform(ks[9], (DEPTH, 2, A_HEADS), f32,
                                    math.log(DT_MIN), math.log(DT_MAX)))
    return {
        'x_prompt': nrm(ks[0], (BATCH, SEQ, D_MODEL), 1.0),
        'x_sample': nrm(ks[1], (DEC_BATCH, DEC_SEQ, D_MODEL), 1.0),
        'c_prompt': nrm(ks[2], (BATCH, D_MODEL), 1.0),
        'c_sample': nrm(ks[3], (DEC_BATCH, D_MODEL), 1.0),
        'w_ada': nrm(ks[4], (DEPTH, D_MODEL, 3 * D_MODEL), ADA_INIT * D_MODEL ** -0.5),
        'b_ada': nrm(ks[5], (DEPTH, 3 * D_MODEL), 0.01),
        'norm_pre': 1.0 + nrm(ks[6], (DEPTH, D_MODEL), 0.02),
        'norm_post': 1.0 + nrm(ks[7], (DEPTH, D_MODEL), 0.02),
        'w_in': nrm(ks[8], (DEPTH, D_MODEL, N_IN), D_MODEL ** -0.5),
        'conv_a': nrm(ks[10], (DEPTH, CONV_K, 3 * A_WIDTH), CONV_K ** -0.5),
        'a_log': jnp.log(jax.random.uniform(ks[11], (DEPTH, 2, A_HEADS), f32, 1.0, 16.0)),
        'dt_bias': dt + jnp.log(-jnp.expm1(-dt)),
        'norm_a': 1.0 + nrm(ks[12], (DEPTH, A_HEAD_DIM), 0.02),
        'w_up_a': nrm(ks[13], (DEPTH, A_WIDTH, D_MODEL), A_WIDTH ** -0.5),
        'w_up_b': nrm(ks[14], (DEPTH, B_GROUP_WIDTH, D_MODEL), B_GROUP_WIDTH ** -0.5),
        'w_out': nrm(ks[15], (DEPTH, D_MODEL, D_MODEL), D_MODEL ** -0.5),
    }


def reference(x_prompt, x_sample, c_prompt, c_sample, w_ada, b_ada, norm_pre, norm_post,
              w_in, conv_a, a_log, dt_bias, norm_a, w_up_a, w_up_b, w_out):
    def trunk(x, c):
        for l in range(DEPTH):
            x = _encoder_layer(x, c, w_ada[l], b_ada[l], norm_pre[l], norm_post[l], w_in[l],
                               conv_a[l], a_log[l], dt_bias[l], norm_a[l], w_up_a[l],
                               w_up_b[l], w_out[l])
        return x

    y_prompt = trunk(x_prompt, c_prompt)
    y_sample = trunk(x_sample, c_sample)
    return (y_prompt, y_sample)
```

```python
import contextlib
import math
import os
import numpy as np
import concourse.bass as bass
import concourse.mybir as mybir
from concourse.bass_utils import run_bass_kernel_spmd

F32 = mybir.dt.float32
BF16 = mybir.dt.bfloat16
AF = mybir.ActivationFunctionType
ALU = mybir.AluOpType

BIG = 30000.0


class Cfg:
    def __init__(self, D=4096, TSEG=4096, DEPTH=4, TB=1024):
        self.D = D
        self.TSEG = TSEG
        self.NSEG = 2
        self.T = 2 * TSEG
        self.DEPTH = DEPTH
        self.KC = D // 128
        self.AH = D // 256
        self.AW = self.AH * 128
        self.BH = D // 512
        self.GW = self.BH * 128
        self.QW = 3 * self.GW
        self.NIN = 4 * self.AW + 4 * self.AH + 3 * self.QW + self.GW + 2 * D
        self.TB = TB
        self.NCH = self.T // 128
        self.CPS = TSEG // 128
        o = 0
        self.split = {}
        for name, w in (("qa", self.AW), ("ka", self.AW), ("va", self.AW), ("za", self.AW),
                        ("ab", 4 * self.AH), ("qb", self.QW), ("kb", self.QW), ("vb", self.QW),
                        ("zb", self.GW), ("gt", 2 * D)):
            self.split[name] = (o, w)
            o += w
        assert o == self.NIN


class _Rec:
    def __init__(self):
        self.call = None

    def __getattr__(self, name):
        def f(*a, **k):
            self.call = (name, a, k)
            return self
        return f


class Tr:
    ENG = ("pe", "dve", "act", "pool", "sp")

    def __init__(self, nc, es, ndma=24, same_engine_sync=True):
        self.nc = nc
        self.prog = {e: [] for e in self.ENG}
        self.sem = {e: es.enter_context(nc.semaphore("sem_" + e)) for e in self.ENG}
        self.cnt = {e: 0 for e in self.ENG}
        self.dsem = [es.enter_context(nc.semaphore("dsem%d" % i)) for i in range(ndma)]
        self.dcnt = [0] * ndma
        self.drr = 0
        self.seen = {e: {} for e in self.ENG}
        self.res = {}
        self.ses = same_engine_sync
        self.nops = 0
        self.es = es
        self.ndma = ndma
        self.fresh_pool = bool(os.environ.get("FRESH_POOL_SEMS"))

    def _st(self, name):
        st = self.res.get(name)
        if st is None:
            st = self.res[name] = {"W": {}, "R": {}, "K": {}}
        return st

    @staticmethod
    def _merge(dst, src):
        for k, v in src.items():
            if dst.get(k, 0) < v:
                dst[k] = v

    def op(self, eng, fn, reads=(), writes=(), dma=False):
        deps = {}
        for (name, key) in reads:
            st = self._st(name)
            self._merge(deps, st["W"])
            if key is None:
                for k in st["K"].values():
                    self._merge(deps, k["W"])
            else:
                k = st["K"].get(key)
                if k:
                    self._merge(deps, k["W"])
        for (name, key) in writes:
            st = self._st(name)
            self._merge(deps, st["W"])
            self._merge(deps, st["R"])
            if key is None:
                for k in st["K"].values():
                    self._merge(deps, k["W"])
                    self._merge(deps, k["R"])
            else:
                k = st["K"].get(key)
                if k:
                    self._merge(deps, k["W"])
                    self._merge(deps, k["R"])
        if dma:
            if self.fresh_pool and eng == "pool":
                self.dsem.append(self.es.enter_context(self.nc.semaphore("psem%d" % len(self.dsem))))
                self.dcnt.append(0)
                i = len(self.dsem) - 1
            else:
                i = self.drr
                self.drr = (self.drr + 1) % self.ndma
            if self.dcnt[i] > 0:
                self._merge(deps, {i: self.dcnt[i]})
        rec = _Rec()
        fn(rec)
        fn = rec.call
        assert fn is not None
        seen = self.seen[eng]
        prog = self.prog[eng]
        for k, v in deps.items():
            if k == eng and (eng == "pe" or not self.ses):
                continue
            if seen.get(k, 0) >= v:
                continue
            seen[k] = v
            prog.append(("w", k, v))
        if dma:
            self.dcnt[i] += 16
            tok = (i, self.dcnt[i])
            prog.append(("o", fn, i, 16))
        else:
            self.cnt[eng] += 1
            tok = (eng, self.cnt[eng])
            prog.append(("o", fn, eng, 1))
        self.nops += 1
        tk, tv = tok
        for (name, key) in reads:
            st = self._st(name)
            if key is None:
                st["R"][tk] = tv
            else:
                kk = st["K"].get(key)
                if kk is None:
                    kk = st["K"][key] = {"W": {}, "R": {}}
                kk["R"][tk] = tv
        for (name, key) in writes:
            st = self._st(name)
            if key is None:
                st["W"] = {tk: tv}
                st["R"] = {}
                st["K"] = {}
            else:
                st["K"][key] = {"W": {tk: tv}, "R": {}}

    def barrier(self):
        for e in self.ENG:
            seen = self.seen[e]
            for k in self.ENG:
                c = self.cnt[k]
                if c > seen.get(k, 0):
                    seen[k] = c
                    self.prog[e].append(("w", k, c))
            for i, c in enumerate(self.dcnt):
                if c > seen.get(i, 0):
                    seen[i] = c
                    self.prog[e].append(("w", i, c))
        self.res = {}

    def emit(self, block):
        decos = {"pe": block.tensor, "dve": block.vector, "act": block.scalar,
                 "pool": block.gpsimd, "sp": block.sync}
        for e in self.ENG:
            prog = self.prog[e]

            def body(engobj, prog=prog):
                for it in prog:
                    if it[0] == "w":
                        k = it[1]
                        sem = self.sem[k] if isinstance(k, str) else self.dsem[k]
                        engobj.wait_ge(sem, it[2])
                    else:
                        _, fn, k, inc = it
                        sem = self.sem[k] if isinstance(k, str) else self.dsem[k]
                        getattr(engobj, fn[0])(*fn[1], **fn[2]).then_inc(sem, inc)

            decos[e](body)


class SB:
    BASE = 16640
    LIMIT = 225000

    def __init__(self, nc):
        self.nc = nc
        self.off = self.BASE
        self.n = 0
        self.marks = []

    def alloc(self, shape, dtype, name="t"):
        nbytes = int(np.prod(shape[1:])) * (4 if dtype == F32 else 2)
        nbytes = (nbytes + 63) // 64 * 64
        assert self.off + nbytes <= self.LIMIT, ("SBUF overflow", name, self.off, nbytes)
        t = self.nc.alloc_sbuf_tensor_at("%s_%d" % (name, self.n), list(shape), dtype, offset=self.off)
        self.n += 1
        self.off += nbytes
        return t

    def push(self):
        self.marks.append(self.off)

    def pop(self):
        self.off = self.marks.pop()


def make_consts():
    i = np.arange(128)
    parts = {}
    parts["ident"] = np.eye(128, dtype=np.float32)
    parts["ones"] = np.ones((128, 128), np.float32)
    parts["tri0"] = (i[:, None] <= i[None, :]).astype(np.float32)
    parts["tri1"] = (i[:, None] >= i[None, :]).astype(np.float32)
    parts["mb0"] = np.where(i[:, None] >= i[None, :], 0.0, BIG).astype(np.float32)
    parts["mb1"] = np.where(i[:, None] <= i[None, :], 0.0, BIG).astype(np.float32)
    parts["st0"] = (i[:, None] > i[None, :]).astype(np.float32)
    parts["st1"] = (i[:, None] < i[None, :]).astype(np.float32)
    perm = np.zeros((128, 128), np.float32)
    perm[(i + 64) % 128, i] = 1.0
    parts["perm"] = perm
    b = np.arange(256)
    parts["band"] = ((i[:, None] <= b[None, :]) & (b[None, :] <= i[:, None] + 128)).astype(np.float32)
    v0 = np.ones((128, 128), np.float32); v0[:64] = 0.0
    v1 = np.ones((128, 128), np.float32); v1[64:] = 0.0
    parts["vfirst"] = v0
    parts["vlast"] = v1
    blk = lambda n: (i[:, None] // n == i[None, :] // n).astype(np.float32)
    parts["bd16"] = blk(16)
    parts["m32"] = blk(32) - blk(16)
    parts["m64"] = blk(64) - blk(32)
    parts["m128"] = blk(128) - blk(64)
    cols = {}
    arrs = []
    o = 0
    for k, a in parts.items():
        cols[k] = (o, a.shape[1])
        arrs.append(a)
        o += a.shape[1]
    return np.concatenate(arrs, axis=1), cols


CONST_ARR, CONST_COLS = make_consts()


def rope_tables(pos):
    half = 64
    inv = (10000.0 ** (-np.arange(half, dtype=np.float32) / half)).astype(np.float32)
    ang = pos.astype(np.float32)[None, :] * inv[:, None]
    cos = np.cos(ang).astype(np.float32)
    sin = np.sin(ang).astype(np.float32)
    c = np.concatenate([cos, cos], axis=0)
    s = np.concatenate([-sin, sin], axis=0)
    return np.ascontiguousarray(c), np.ascontiguousarray(s)


def build_program(cfg, debug=False, phases=("p0", "p1", "p2", "p3", "p4"), same_engine_sync=True, inject=False):
    nc = bass.Bass("TRN2", target_bir_lowering=False)
    es = contextlib.ExitStack()
    D, T, TSEG, KC, DEPTH = cfg.D, cfg.T, cfg.TSEG, cfg.KC, cfg.DEPTH
    AH, AW, BH, GW, QW, NIN = cfg.AH, cfg.AW, cfg.BH, cfg.GW, cfg.QW, cfg.NIN
    NCH, CPS, TB = cfg.NCH, cfg.CPS, cfg.TB

    def din(name, shape, dt=F32):
        return nc.dram_tensor(name, list(shape), dt, kind="ExternalInput").ap()

    def dscr(name, shape, dt):
        kind = "ExternalOutput" if (debug is True or (debug and name in debug)) else "Internal"
        return nc.dram_tensor(name, list(shape), dt, kind=kind).ap()

    x_in = din("x", [T, D])
    cT_in = din("cT", [128, KC, 2])
    link_in = din("link", [128, 2])
    cst_in = din("cst", list(CONST_ARR.shape))
    ropec_in = din("ropec", [128, T])
    ropes_in = din("ropes", [128, T])
    w_ada = din("w_ada", [DEPTH, D, 3 * D])
    b_ada = din("b_ada", [DEPTH, 3 * D])
    norm_pre = din("norm_pre", [DEPTH, D])
    norm_post = din("norm_post", [DEPTH, D])
    w_in = din("w_in", [DEPTH, D, NIN])
    conv_in = din("conv_t", [DEPTH, 128, 3 * AH * 5])
    alog_in = din("a_log", [DEPTH, 2 * AH])
    dtb_in = din("dt_bias", [DEPTH, 2 * AH])
    norm_a = din("norm_a", [DEPTH, 128])
    w_up_a = din("w_up_a", [DEPTH, AW, D])
    w_up_b = din("w_up_b", [DEPTH, GW, D])
    w_out = din("w_out", [DEPTH, D, D])
    y_out = nc.dram_tensor("y", [T, D], F32, kind="ExternalOutput").ap()

    modd = dscr("modd", [DEPTH, 2, 3 * D], F32)
    xbuf = [dscr("xbuf0", [T, D], F32), dscr("xbuf1", [T, D], F32)]
    qaT = dscr("qaT", [AW, T], BF16)
    kaT = dscr("kaT", [AW, T], BF16)
    vaT = dscr("vaT", [AW, T], BF16)
    zaS = dscr("zaS", [T, AW], BF16)
    abd = dscr("abd", [T, 4 * AH], F32)
    qbT = dscr("qbT", [QW, T], BF16)
    kbT = dscr("kbT", [QW, T], BF16)
    vbS = dscr("vbS", [T, QW], BF16)
    zbT = dscr("zbT", [GW, T], BF16)
    gT = dscr("gT", [2 * D, T], BF16)
    if inject:
        yaT = din("yaT", [AW, T], BF16)
        ybT = din("ybT", [GW, T], BF16)
    else:
        yaT = dscr("yaT", [AW, T], BF16)
        ybT = dscr("ybT", [GW, T], BF16)
    outd = dscr("outd", [T, D], F32)

    tr = Tr(nc, es, same_engine_sync=same_engine_sync)
    sb = SB(nc)
    op = tr.op

    psb = [nc.alloc_psum_tensor("psb%d" % i, [128, 512], F32) for i in range(8)]

    def PS(i):
        return ("ps%d" % i, None)

    NCST = CONST_ARR.shape[1]
    cst = sb.alloc([128, NCST], F32, "cst")
    op("sp", lambda e: e.dma_start(out=cst[:], in_=cst_in), [], [("cst", None)], dma=True)

    def C(name):
        o, n = CONST_COLS[name]
        return cst[:, o:o + n]

    ident_bf = sb.alloc([128, 128], BF16, "identbf")
    op("dve", lambda e: e.tensor_copy(out=ident_bf[:], in_=C("ident")), [("cst", None)], [("identbf", None)])
    linkt = sb.alloc([128, 2], F32, "link")
    op("sp", lambda e: e.dma_start(out=linkt[:], in_=link_in), [], [("link", None)], dma=True)
    RC = [("cst", None), ("identbf", None), ("link", None)]
    sb.push()

    if "p0" in phases:
        sb.push()
        siluc = sb.alloc([128, KC, 2], F32, "siluc")
        op("sp", lambda e: e.dma_start(out=siluc[:], in_=cT_in), [], [("siluc", None)], dma=True)
        op("act", lambda e: e.activation(out=siluc[:], in_=siluc[:], func=AF.Silu),
           [("siluc", None)], [("siluc", None)])
        NB = 512
        wt = [sb.alloc([128, KC, NB], F32, "wada%d" % i) for i in range(2)]
        brow = [sb.alloc([2, NB], F32, "brow%d" % i) for i in range(2)]
        mrow = [sb.alloc([2, NB], F32, "mrow%d" % i) for i in range(2)]
        it = 0
        for l in range(DEPTH):
            for nb in range(3 * D // NB):
                w = wt[it % 2]
                wn = "wada%d" % (it % 2)
                br = brow[it % 2]
                brn = "brow%d" % (it % 2)
                mr = mrow[it % 2]
                mrn = "mrow%d" % (it % 2)
                src = w_ada[l, :, nb * NB:(nb + 1) * NB].rearrange("(kc p) n -> p kc n", p=128)
                op("sp", lambda e: e.dma_start(out=w[:], in_=src), [], [(wn, None)], dma=True)
                op("sp", lambda e: e.dma_start(out=br[:], in_=b_ada[l:l + 1, nb * NB:(nb + 1) * NB].broadcast_to([2, NB])),
                   [], [(brn, None)], dma=True)
                bank = it % 2
                for kc in range(KC):
                    op("pe", lambda e: e.matmul(psb[bank][0:2, :], lhsT=siluc[:, kc, :], rhs=w[:, kc, :],
                                                start=(kc == 0), stop=(kc == KC - 1)),
                       [("siluc", None), (wn, None)], [PS(bank)])
                op("dve", lambda e: e.tensor_tensor(out=mr[:], in0=psb[bank][0:2, :], in1=br[:], op=ALU.add),
                   [PS(bank), (brn, None)], [(mrn, None)])
                op("sp", lambda e: e.dma_start(out=modd[l, :, nb * NB:(nb + 1) * NB], in_=mr[:]),
                   [(mrn, None)], [("modd", (l, nb))], dma=True)
                it += 1
        tr.barrier()
        sb.pop()

    for l in range(DEPTH):
        xsrc = x_in if l == 0 else xbuf[(l - 1) % 2]
        xdst = y_out if l == DEPTH - 1 else xbuf[l % 2]
        if "p1" in phases:
            phase1(nc, tr, sb, cfg, l, xsrc, locals())
        if "p2" in phases:
            phase2(nc, tr, sb, cfg, l, locals())
        if "p3" in phases:
            phase3(nc, tr, sb, cfg, l, locals())
        if "p4" in phases:
            phase4(nc, tr, sb, cfg, l, xsrc, xdst, locals())

    tr.barrier()
    block = es.enter_context(nc.Block())
    tr.emit(block)
    es.close()
    return nc, tr


def phase1(nc, tr, sb, cfg, l, xsrc, g):
    op = tr.op
    D, T, TSEG, KC = cfg.D, cfg.T, cfg.TSEG, cfg.KC
    TB = cfg.TB
    psb, C, ident_bf = g["psb"], g["C"], g["ident_bf"]
    PS = g["PS"]
    modd, norm_pre, w_in = g["modd"], g["norm_pre"], g["w_in"]
    ropec_in, ropes_in = g["ropec_in"], g["ropes_in"]
    dest = {"qa": g["qaT"], "ka": g["kaT"], "va": g["vaT"], "za": g["zaS"], "ab": g["abd"],
            "qb": g["qbT"], "kb": g["kbT"], "vb": g["vbS"], "zb": g["zbT"], "gt": g["gT"]}
    kind = {"qa": "fm", "ka": "fm", "va": "fm", "za": "tm", "ab": "tm", "qb": "fm", "kb": "fm",
            "vb": "tm", "zb": "fm", "gt": "fm"}
    NT = TB // 128
    TG = min(8, KC)

    for tb in range(T // TB):
        tok0 = tb * TB
        seg = tok0 // TSEG
        sb.push()
        hT = sb.alloc([128, KC, TB], BF16, "hT")
        sb.push()
        A = sb.alloc([128, D], F32, "A")
        B = sb.alloc([128, D], F32, "B")
        xt = [sb.alloc([128, D], F32, "xt%d" % i) for i in range(2)]
        hb = sb.alloc([128, D], BF16, "hb")
        junk = sb.alloc([128, D], BF16, "junk")
        st = sb.alloc([128, 4 * NT], F32, "st")
        op("sp", lambda e: e.dma_start(out=A[:], in_=modd[l, seg:seg + 1, D:2 * D].broadcast_to([128, D])),
           [("modd", None)], [("A", None)], dma=True)
        op("sp", lambda e: e.dma_start(out=xt[0][:], in_=norm_pre[l:l + 1, :].broadcast_to([128, D])),
           [], [("xt0", None)], dma=True)
        op("sp", lambda e: e.dma_start(out=B[:], in_=modd[l, seg:seg + 1, 0:D].broadcast_to([128, D])),
           [("modd", None)], [("B", None)], dma=True)
        op("dve", lambda e: e.scalar_tensor_tensor(out=A[:], in0=A[:], scalar=1.0, in1=xt[0][:],
                                                     op0=ALU.add, op1=ALU.mult),
           [("A", None), ("xt0", None)], [("A", None)])
        for tt in range(NT):
            x = xt[tt % 2]
            xn = "xt%d" % (tt % 2)
            r0 = tok0 + tt * 128
            op("sp", lambda e, x=x, r0=r0: e.dma_start(out=x[:], in_=xsrc[r0:r0 + 128, :]),
               [("xres", None)], [(xn, None)], dma=True)
            ss = st[:, 4 * tt:4 * tt + 1]
            v1 = st[:, 4 * tt + 1:4 * tt + 2]
            v2 = st[:, 4 * tt + 2:4 * tt + 3]
            rs = st[:, 4 * tt + 3:4 * tt + 4]
            op("act", lambda e, x=x, ss=ss: e.activation(out=junk[:], in_=x[:], func=AF.Square, accum_out=ss),
               [(xn, None)], [("junk", None), ("st", tt)])
            op("dve", lambda e, ss=ss, v1=v1: e.tensor_scalar(out=v1, in0=ss, scalar1=1.0 / D, scalar2=1e-6,
                                                              op0=ALU.mult, op1=ALU.add),
               [("st", tt)], [("st", tt)])
            op("act", lambda e, v1=v1, v2=v2: e.activation(out=v2, in_=v1, func=AF.Sqrt),
               [("st", tt)], [("st", tt)])
            op("dve", lambda e, v2=v2, rs=rs: e.reciprocal(out=rs, in_=v2), [("st", tt)], [("st", tt)])
            op("dve", lambda e, x=x, rs=rs: e.scalar_tensor_tensor(out=x[:], in0=x[:], scalar=rs, in1=A[:],
                                                                   op0=ALU.mult, op1=ALU.mult),
               [(xn, None), ("st", tt), ("A", None)], [(xn, None)])
            op("dve", lambda e, x=x: e.tensor_tensor(out=hb[:], in0=x[:], in1=B[:], op=ALU.add),
               [(xn, None), ("B", None)], [("hb", None)])
            for g0 in range(0, KC, TG):
                bank = (g0 // TG) % 4
                pv = psb[bank][:].bitcast(BF16).rearrange("p (j t) -> p j t", j=8)
                for j in range(TG):
                    kc = g0 + j
                    op("pe", lambda e, pv=pv, j=j, kc=kc: e.transpose(
                        out=pv[:, j, :], in_=hb[:, kc * 128:(kc + 1) * 128], identity=ident_bf[:]),
                       [("hb", None), ("identbf", None)], [PS(bank)])
                op("act", lambda e, pv=pv, g0=g0, tt=tt: e.activation(
                    out=hT[:, g0:g0 + TG, tt * 128:(tt + 1) * 128], in_=pv[:, 0:TG, :], func=AF.Copy),
                   [PS(bank)], [("hT", tt)])
        tr.barrier()
        sb.pop()
        sb.push()
        NWB = 3
        wb = [sb.alloc([128, KC, 512], BF16, "wb%d" % i) for i in range(NWB)]
        cosb = sb.alloc([128, TB], F32, "cosb")
        sinb = sb.alloc([128, TB], F32, "sinb")
        NOB = 4
        ob = [sb.alloc([128, 512], F32, "ob%d" % i) for i in range(NOB)]
        qf = [sb.alloc([128, 512], F32, "qf%d" % i) for i in range(2)]
        t1 = [sb.alloc([128, 512], F32, "t1_%d" % i) for i in range(2)]
        op("sp", lambda e: e.dma_start(out=cosb[:], in_=ropec_in[:, tok0:tok0 + TB]), [], [("cosb", None)], dma=True)
        op("sp", lambda e: e.dma_start(out=sinb[:], in_=ropes_in[:, tok0:tok0 + TB]), [], [("sinb", None)], dma=True)
        wi = 0
        oi = 0
        pi = 0
        ri = 0
        for name in ("qa", "ka", "va", "za", "ab", "qb", "kb", "vb", "zb", "gt"):
            c0s, wdt = cfg.split[name]
            dst = dest[name]
            for cb in range(0, wdt, 512):
                ncol = min(512, wdt - cb)
                w = wb[wi % NWB]
                wn = "wb%d" % (wi % NWB)
                wi += 1
                src = w_in[l, :, c0s + cb:c0s + cb + ncol].rearrange("(kc p) n -> p kc n", p=128)
                op("pool", lambda e, w=w, src=src, ncol=ncol: e.dma_start(out=w[:, :, 0:ncol], in_=src),
                   [], [(wn, None)], dma=True)
                if kind[name] == "fm":
                    for m in range(ncol // 128):
                        row0 = cb + m * 128
                        for th in range(TB // 512):
                            bank = pi % 6
                            pi += 1
                            for kc in range(KC):
                                op("pe", lambda e, w=w, kc=kc, m=m, th=th, bank=bank: e.matmul(
                                    psb[bank][:], lhsT=w[:, kc, m * 128:(m + 1) * 128],
                                    rhs=hT[:, kc, th * 512:(th + 1) * 512],
                                    start=(kc == 0), stop=(kc == KC - 1)),
                                   [(wn, None), ("hT", None)], [PS(bank)])
                            o = ob[oi % NOB]
                            on = "ob%d" % (oi % NOB)
                            oi += 1
                            obf = o[:].bitcast(BF16)[:, 0:512]
                            if name in ("qb", "kb"):
                                q = qf[ri % 2]
                                qn = "qf%d" % (ri % 2)
                                tt1 = t1[ri % 2]
                                tn = "t1_%d" % (ri % 2)
                                rb = 6 + (ri % 2)
                                ri += 1
                                cs = cosb[:, th * 512:(th + 1) * 512]
                                sn = sinb[:, th * 512:(th + 1) * 512]
                                op("act", lambda e, q=q, bank=bank: e.activation(out=q[:], in_=psb[bank][:], func=AF.Copy),
                                   [PS(bank)], [(qn, None)])
                                op("pe", lambda e, q=q, rb=rb: e.matmul(psb[rb][:], lhsT=C("perm"), rhs=q[:],
                                                                        start=True, stop=True),
                                   [(qn, None), ("cst", None)], [PS(rb)])
                                op("dve", lambda e, q=q, tt1=tt1, cs=cs: e.tensor_tensor(out=tt1[:], in0=q[:], in1=cs, op=ALU.mult),
                                   [(qn, None), ("cosb", None)], [(tn, None)])
                                op("dve", lambda e, q=q, rb=rb, sn=sn: e.tensor_tensor(out=q[:], in0=psb[rb][:], in1=sn, op=ALU.mult),
                                   [PS(rb), ("sinb", None)], [(qn, None)])
                                op("dve", lambda e, q=q, tt1=tt1, obf=obf: e.tensor_tensor(out=obf, in0=q[:], in1=tt1[:], op=ALU.add),
                                   [(qn, None), (tn, None)], [(on, None)])
                            else:
                                fn = {"zb": AF.Silu, "gt": AF.Sigmoid}.get(name, AF.Copy)
                                op("act", lambda e, obf=obf, bank=bank, fn=fn: e.activation(out=obf, in_=psb[bank][:], func=fn),
                                   [PS(bank)], [(on, None)])
                            t0 = tok0 + th * 512
                            op("sp", lambda e, obf=obf, dst=dst, row0=row0, t0=t0: e.dma_start(
                                out=dst[row0:row0 + 128, t0:t0 + 512], in_=obf),
                               [(on, None)], [(name, (row0, t0))], dma=True)
                else:
                    for tt in range(NT):
                        bank = pi % 6
                        pi += 1
                        for kc in range(KC):
                            op("pe", lambda e, w=w, kc=kc, tt=tt, bank=bank, ncol=ncol: e.matmul(
                                psb[bank][:, 0:ncol], lhsT=hT[:, kc, tt * 128:(tt + 1) * 128],
                                rhs=w[:, kc, 0:ncol], start=(kc == 0), stop=(kc == KC - 1)),
                               [(wn, None), ("hT", None)], [PS(bank)])
                        o = ob[oi % NOB]
                        on = "ob%d" % (oi % NOB)
                        oi += 1
                        if name == "ab":
                            ov = o[:, 0:ncol]
                        else:
                            ov = o[:].bitcast(BF16)[:, 0:ncol]
                        fn = AF.Silu if name == "za" else AF.Copy
                        op("act", lambda e, ov=ov, bank=bank, fn=fn, ncol=ncol: e.activation(
                            out=ov, in_=psb[bank][:, 0:ncol], func=fn), [PS(bank)], [(on, None)])
                        t0 = tok0 + tt * 128
                        op("sp", lambda e, ov=ov, dst=dst, t0=t0, cb=cb, ncol=ncol: e.dma_start(
                            out=dst[t0:t0 + 128, cb:cb + ncol], in_=ov), [(on, None)], [(name, (t0, cb))], dma=True)
        tr.barrier()
        sb.pop()
        sb.pop()


def phase2(nc, tr, sb, cfg, l, g):
    op = tr.op
    T, TSEG, AH, NCH, CPS = cfg.T, cfg.TSEG, cfg.AH, cfg.NCH, cfg.CPS
    NS = 2 * AH
    psb, PS, C, ident_bf, linkt = g["psb"], g["PS"], g["C"], g["ident_bf"], g["linkt"]
    qaT, kaT, vaT, zaS, abd, yaT = g["qaT"], g["kaT"], g["vaT"], g["zaS"], g["abd"], g["yaT"]
    conv_in, alog_in, dtb_in, norm_a = g["conv_in"], g["alog_in"], g["dtb_in"], g["norm_a"]
    AX = mybir.AxisListType.X
    RCST = [("cst", None)]

    def bf8(bank):
        return psb[bank][:].bitcast(BF16).rearrange("p (j t) -> p j t", j=8)

    sb.push()
    convw = sb.alloc([128, 3 * AH * 5], F32, "convw")
    normw = sb.alloc([128, 128], F32, "normw")
    epsb = sb.alloc([128, 1], F32, "epsb")
    graw = sb.alloc([128, NS, NCH], F32, "graw")
    beta = sb.alloc([128, NS, NCH], F32, "beta")
    gcum = sb.alloc([128, NS, NCH], F32, "gcum")
    glast = sb.alloc([128, NS, NCH], F32, "glast")
    op("sp", lambda e: e.dma_start(out=convw[:], in_=conv_in[l]), [], [("convw", None)], dma=True)
    op("sp", lambda e: e.dma_start(out=normw[:], in_=norm_a[l:l + 1, :].broadcast_to([128, 128])), [], [("normw", None)], dma=True)
    op("pool", lambda e: e.memset(epsb[:], 1e-6), [], [("epsb", None)])
    sb.push()
    abt = sb.alloc([128, NCH, 2 * NS], F32, "abt")
    alb = sb.alloc([128, NS], F32, "alb")
    dtb = sb.alloc([128, NS], F32, "dtb")
    tmp = sb.alloc([128, NS, NCH], F32, "p2tmp")
    op("sp", lambda e: e.dma_start(out=abt[:], in_=abd.rearrange("(n p) c -> p n c", p=128)), [], [("abt", None)], dma=True)
    op("sp", lambda e: e.dma_start(out=alb[:], in_=alog_in[l:l + 1, :].broadcast_to([128, NS])), [], [("alb", None)], dma=True)
    op("sp", lambda e: e.dma_start(out=dtb[:], in_=dtb_in[l:l + 1, :].broadcast_to([128, NS])), [], [("dtb", None)], dma=True)
    op("act", lambda e: e.activation(out=alb[:], in_=alb[:], func=AF.Exp), [("alb", None)], [("alb", None)])
    op("dve", lambda e: e.tensor_scalar(out=alb[:], in0=alb[:], scalar1=-1.0, scalar2=None, op0=ALU.mult),
       [("alb", None)], [("alb", None)])
    a_view = abt[:, :, 0:NS].rearrange("p n c -> p c n")
    b_view = abt[:, :, NS:2 * NS].rearrange("p n c -> p c n")
    op("dve", lambda e: e.tensor_tensor(out=tmp[:], in0=a_view, in1=dtb[:].unsqueeze(2).broadcast_to([128, NS, NCH]), op=ALU.add),
       [("abt", None), ("dtb", None)], [("p2tmp", None)])
    op("act", lambda e: e.activation(out=tmp[:], in_=tmp[:], func=AF.Exp), [("p2tmp", None)], [("p2tmp", None)])
    op("act", lambda e: e.activation(out=tmp[:], in_=tmp[:], func=AF.Ln, bias=1.0), [("p2tmp", None)], [("p2tmp", None)])
    op("dve", lambda e: e.tensor_tensor(out=graw[:], in0=tmp[:], in1=alb[:].unsqueeze(2).broadcast_to([128, NS, NCH]), op=ALU.mult),
       [("p2tmp", None), ("alb", None)], [("graw", None)])
    op("act", lambda e: e.activation(out=beta[:], in_=b_view, func=AF.Sigmoid), [("abt", None)], [("beta", None)])
    HB = max(1, 512 // NCH)
    bi = 0
    for dr in range(2):
        for h0 in range(0, AH, HB):
            nh = min(HB, AH - h0)
            bank = bi % 4
            bi += 1
            c0 = dr * AH + h0
            rhs = graw[:, c0:c0 + nh, :].rearrange("p a n -> p (a n)")
            op("pe", lambda e: e.matmul(psb[bank][:, 0:nh * NCH], lhsT=C("tri%d" % dr), rhs=rhs, start=True, stop=True),
               [("graw", None)] + RCST, [PS(bank)])
            op("act", lambda e: e.activation(out=gcum[:, c0:c0 + nh, :].rearrange("p a n -> p (a n)"),
                                             in_=psb[bank][:, 0:nh * NCH], func=AF.Copy), [PS(bank)], [("gcum", c0)])
            bank = bi % 4
            bi += 1
            op("pe", lambda e: e.matmul(psb[bank][:, 0:nh * NCH], lhsT=C("ones"), rhs=rhs, start=True, stop=True),
               [("graw", None)] + RCST, [PS(bank)])
            op("act", lambda e: e.activation(out=glast[:, c0:c0 + nh, :].rearrange("p a n -> p (a n)"),
                                             in_=psb[bank][:, 0:nh * NCH], func=AF.Copy), [PS(bank)], [("glast", c0)])
    tr.barrier()
    sb.pop()

    for h in range(AH):
        sb.push()
        q_fm = sb.alloc([128, T], BF16, "q_fm")
        k_fm = sb.alloc([128, T], BF16, "k_fm")
        k_tm = sb.alloc([128, NCH, 128], BF16, "k_tm")
        v_tm = sb.alloc([128, NCH, 128], BF16, "v_tm")
        o_tm = sb.alloc([128, NCH, 128], F32, "o_tm")
        hs = sb.alloc([128, 6, 2, NCH], F32, "hs")
        Sst = [sb.alloc([128, 128], F32, "S%d" % i) for i in range(2)]
        sb.push()
        xin = sb.alloc([128, TSEG + 4], BF16, "xin")
        accb = sb.alloc([128, TSEG], F32, "accb")
        sq = sb.alloc([128, TSEG], F32, "sq")
        vtmp = sb.alloc([128, TSEG], BF16, "vtmp")
        sd = [sb.alloc([128, 512], F32, "sd%d" % i) for i in range(2)]
        tbi = 0
        nbi = 0
        for ti, src in enumerate((qaT, kaT, vaT)):
            for seg in range(2):
                t0 = seg * TSEG
                r0 = h * 128
                op("sp", lambda e: e.dma_start(out=xin[:, 2:2 + TSEG], in_=src[r0:r0 + 128, t0:t0 + TSEG]),
                   [], [("xin", "body")], dma=True)
                if seg == 0:
                    op("pool", lambda e: e.memset(xin[:, 0:2], 0.0), [], [("xin", "l")])
                    op("sp", lambda e: e.dma_start(out=xin[:, TSEG + 2:TSEG + 4], in_=src[r0:r0 + 128, TSEG:TSEG + 2]),
                       [], [("xin", "r")], dma=True)
                    op("dve", lambda e: e.tensor_scalar(out=xin[:, TSEG + 2:TSEG + 4], in0=xin[:, TSEG + 2:TSEG + 4],
                                                        scalar1=linkt[:, 0:1], scalar2=None, op0=ALU.mult),
                       [("xin", "r"), ("link", None)], [("xin", "r")])
                else:
                    op("pool", lambda e: e.memset(xin[:, TSEG + 2:TSEG + 4], 0.0), [], [("xin", "r")])
                    op("sp", lambda e: e.dma_start(out=xin[:, 0:2], in_=src[r0:r0 + 128, TSEG - 2:TSEG]),
                       [], [("xin", "l")], dma=True)
                    op("dve", lambda e: e.tensor_scalar(out=xin[:, 0:2], in0=xin[:, 0:2],
                                                        scalar1=linkt[:, 0:1], scalar2=None, op0=ALU.mult),
                       [("xin", "l"), ("link", None)], [("xin", "l")])
                w0 = (ti * AH + h) * 5
                op("dve", lambda e: e.tensor_scalar(out=accb[:], in0=xin[:, 0:TSEG], scalar1=convw[:, w0:w0 + 1],
                                                    scalar2=None, op0=ALU.mult),
                   [("xin", None), ("convw", None)], [("accb", None)])
                for j in range(1, 5):
                    op("dve", lambda e: e.scalar_tensor_tensor(out=accb[:], in0=xin[:, j:j + TSEG],
                                                               scalar=convw[:, w0 + j:w0 + j + 1], in1=accb[:],
                                                               op0=ALU.mult, op1=ALU.add),
                       [("xin", None), ("convw", None), ("accb", None)], [("accb", None)])
                if ti == 2:
                    op("act", lambda e: e.activation(out=vtmp[:], in_=accb[:], func=AF.Silu), [("accb", None)], [("vtmp", None)])
                    for n0 in range(0, CPS, 8):
                        bank = tbi % 2
                        tbi += 1
                        pv = bf8(bank)
                        for j in range(8):
                            n = n0 + j
                            op("pe", lambda e: e.transpose(out=pv[:, j, :], in_=vtmp[:, n * 128:(n + 1) * 128], identity=ident_bf[:]),
                               [("vtmp", None), ("identbf", None)], [PS(bank)])
                        op("act", lambda e: e.activation(out=v_tm[:, seg * CPS + n0:seg * CPS + n0 + 8, :], in_=pv[:, 0:8, :], func=AF.Copy),
                           [PS(bank)], [("v_tm", seg * CPS + n0)])
                else:
                    dstt = q_fm if ti == 0 else k_fm
                    dn = "q_fm" if ti == 0 else "k_fm"
                    qscale = 128.0 ** -0.5 if ti == 0 else 1.0
                    op("act", lambda e: e.activation(out=accb[:], in_=accb[:], func=AF.Silu), [("accb", None)], [("accb", None)])
                    op("act", lambda e: e.activation(out=sq[:], in_=accb[:], func=AF.Square), [("accb", None)], [("sq", None)])
                    for blk in range(TSEG // 512):
                        bank = 2 + nbi % 2
                        sdt = sd[nbi % 2]
                        sdn = "sd%d" % (nbi % 2)
                        nbi += 1
                        op("pe", lambda e: e.matmul(psb[bank][:], lhsT=C("ones"), rhs=sq[:, blk * 512:(blk + 1) * 512], start=True, stop=True),
                           [("sq", None)] + RCST, [PS(bank)])
                        op("act", lambda e: e.activation(out=sdt[:], in_=psb[bank][:], func=AF.Sqrt, bias=epsb[:, 0:1]),
                           [PS(bank), ("epsb", None)], [(sdn, None)])
                        op("dve", lambda e: e.reciprocal(out=sdt[:], in_=sdt[:]), [(sdn, None)], [(sdn, None)])
                        op("dve", lambda e: e.scalar_tensor_tensor(out=dstt[:, t0 + blk * 512:t0 + (blk + 1) * 512],
                                                                   in0=accb[:, blk * 512:(blk + 1) * 512], scalar=qscale,
                                                                   in1=sdt[:], op0=ALU.mult, op1=ALU.mult),
                           [("accb", None), (sdn, None)], [(dn, (seg, blk))])
        for n0 in range(0, NCH, 8):
            bank = tbi % 2
            tbi += 1
            pv = bf8(bank)
            for j in range(8):
                n = n0 + j
                op("pe", lambda e: e.transpose(out=pv[:, j, :], in_=k_fm[:, n * 128:(n + 1) * 128], identity=ident_bf[:]),
                   [("k_fm", None), ("identbf", None)], [PS(bank)])
            op("act", lambda e: e.activation(out=k_tm[:, n0:n0 + 8, :], in_=pv[:, 0:8, :], func=AF.Copy),
               [PS(bank)], [("k_tm", n0)])
        gc = gcum[:, h:NS:AH, :]
        gl = glast[:, h:NS:AH, :]
        bt = beta[:, h:NS:AH, :]
        op("act", lambda e: e.activation(out=hs[:, 0], in_=gc, func=AF.Exp), [("gcum", None)], [("hs", 0)])
        op("dve", lambda e: e.tensor_tensor(out=hs[:, 5], in0=gl, in1=gc, op=ALU.subtract), [("gcum", None), ("glast", None)], [("hs", 5)])
        op("act", lambda e: e.activation(out=hs[:, 1], in_=hs[:, 5], func=AF.Exp), [("hs", 5)], [("hs", 1)])
        op("act", lambda e: e.activation(out=hs[:, 2], in_=gl, func=AF.Exp), [("glast", None)], [("hs", 2)])
        op("dve", lambda e: e.tensor_tensor(out=hs[:, 3], in0=bt, in1=hs[:, 0], op=ALU.mult), [("beta", None), ("hs", 0)], [("hs", 3)])
        op("dve", lambda e: e.tensor_scalar(out=hs[:, 4], in0=bt, scalar1=-1.0, scalar2=None, op0=ALU.mult), [("beta", None)], [("hs", 4)])
        op("pool", lambda e: e.memset(o_tm[:], 0.0), [], [("o_tm", None)])
        for dr in range(2):
            op("pool", lambda e: e.memset(Sst[dr][:], 0.0), [], [("S%d" % dr, None)])
        tr.barrier()
        sb.pop()
        sb.push()
        NI = 2
        NU = 2 * NI

        def ub(dt_, nm):
            return sb.alloc([128, NU, 128], dt_, nm)

        GbT = ub(F32, "GbT"); dec = ub(F32, "dec"); ubuf = ub(F32, "ubuf")
        decs = GbT
        avs = sb.alloc([128, 2, 128], F32, "avs"); otmp = sb.alloc([128, 2, 128], F32, "otmp")
        NN = ub(F32, "NN"); NTT = ub(F32, "NTT")
        Xp = [ub(F32, "Xp0"), ub(F32, "Xp1")]
        XTp = [ub(F32, "XTp0"), ub(F32, "XTp1")]
        Tn = [ub(F32, "T0"), ub(F32, "T1")]
        Tt = [ub(F32, "Tt0"), ub(F32, "Tt1")]
        EE = ub(F32, "EE"); EET = ub(F32, "EET"); A1s = ub(F32, "A1s"); B1s = ub(F32, "B1s")
        attn = ub(BF16, "attn"); attnT = ub(BF16, "attnT")
        vn = ub(BF16, "vn"); qf32 = ub(F32, "qf32")
        vb = Xp[1]; vbn = "Xp1"
        kbeg = XTp[1]; kbegn = "XTp1"
        kdec = EE; kdecn = "EE"
        wT = A1s; wTn = "A1s"
        vnf = B1s; vnfn = "B1s"
        pbank = [0]

        def pair():
            p = pbank[0]
            pbank[0] = (p + 1) % 3
            return (2 * p, 2 * p + 1)

        def col(t_, c, n):
            return t_[:, c, n:n + 1]

        def fl(t_, dr):
            return t_[:, dr * NI:(dr + 1) * NI, :].rearrange("p a b -> p (a b)")

        def bc(name):
            return C(name).unsqueeze(1).broadcast_to([128, NU, 128])

        W_ = NI * 128
        for gk in range(NCH // NI):
            units = []
            for dr in range(2):
                for i in range(NI):
                    k_ = gk * NI + i
                    n = k_ if dr == 0 else NCH - 1 - k_
                    units.append((dr * NI + i, dr, i, n))
            bp = pair()
            for (u, dr, i, n) in units:
                op("pool", lambda e: e.tensor_scalar(out=GbT[:, u, :], in0=C("ones"), scalar1=col(graw, dr * AH + h, n),
                                                     scalar2=None, op0=ALU.mult),
                   [("graw", None)] + RCST, [("GbT", u)])
            for (u, dr, i, n) in units:
                op("pe", lambda e: e.matmul(psb[bp[dr]][:, i * 128:(i + 1) * 128], lhsT=GbT[:, u, :], rhs=C("tri%d" % dr),
                                            start=True, stop=True), [("GbT", u)] + RCST, [PS(bp[dr])])
            for (u, dr, i, n) in units:
                op("dve", lambda e: e.scalar_tensor_tensor(out=dec[:, u, :], in0=psb[bp[dr]][:, i * 128:(i + 1) * 128],
                                                           scalar=col(gcum, dr * AH + h, n), in1=C("mb%d" % dr),
                                                           op0=ALU.subtract, op1=ALU.add),
                   [PS(bp[dr]), ("gcum", None)] + RCST, [("dec", u)])
            for dr in range(2):
                op("act", lambda e: e.activation(out=dec[:, dr * NI:(dr + 1) * NI, :], in_=dec[:, dr * NI:(dr + 1) * NI, :], func=AF.Exp, scale=-1.0),
                   [("dec", None)], [("dec", None)])
            for dr in range(2):
                op("pool", lambda e: e.tensor_tensor(out=decs[:, dr * NI:(dr + 1) * NI, :], in0=dec[:, dr * NI:(dr + 1) * NI, :],
                                                     in1=C("st%d" % dr).unsqueeze(1).broadcast_to([128, NI, 128]), op=ALU.mult),
                   [("dec", None)] + RCST, [("GbT", None)])
            bk = pair()
            bq = pair()
            for (u, dr, i, n) in units:
                kc_ = k_fm[:, n * 128:(n + 1) * 128]
                op("pe", lambda e: e.matmul(psb[bk[dr]][:, i * 128:(i + 1) * 128], lhsT=kc_, rhs=kc_, start=True, stop=True),
                   [("k_fm", None)], [PS(bk[dr])])
            for (u, dr, i, n) in units:
                op("pe", lambda e: e.matmul(psb[bq[dr]][:, i * 128:(i + 1) * 128], lhsT=q_fm[:, n * 128:(n + 1) * 128],
                                            rhs=k_fm[:, n * 128:(n + 1) * 128], start=True, stop=True),
                   [("k_fm", None), ("q_fm", None)], [PS(bq[dr])])
            for (u, dr, i, n) in units:
                op("dve", lambda e: e.scalar_tensor_tensor(out=NN[:, u, :], in0=psb[bk[dr]][:, i * 128:(i + 1) * 128],
                                                           scalar=hs[:, 4, dr, n:n + 1], in1=decs[:, u, :],
                                                           op0=ALU.mult, op1=ALU.mult),
                   [PS(bk[dr]), ("hs", None), ("GbT", None)], [("NN", u)])
            for dr in range(2):
                op("dve", lambda e: e.tensor_tensor(out=fl(attn, dr), in0=psb[bq[dr]][:, 0:W_], in1=fl(dec, dr), op=ALU.mult),
                   [PS(bq[dr]), ("dec", None)], [("attn", dr)])
            bt_ = pair()
            bt2 = pair()[0]
            for (u, dr, i, n) in units:
                op("pe", lambda e: e.transpose(out=psb[bt_[dr]][:, i * 128:(i + 1) * 128], in_=NN[:, u, :], identity=C("ident")),
                   [("NN", None)] + RCST, [PS(bt_[dr])])
            for (u, dr, i, n) in units:
                op("pe", lambda e: e.transpose(out=bf8(bt2)[:, u, :], in_=attn[:, u, :], identity=ident_bf[:]),
                   [("attn", None), ("identbf", None)], [PS(bt2)])
            for dr in range(2):
                op("act", lambda e: e.activation(out=fl(NTT, dr), in_=psb[bt_[dr]][:, 0:W_], func=AF.Copy), [PS(bt_[dr])], [("NTT", dr)])
            op("act", lambda e: e.activation(out=attnT[:], in_=bf8(bt2)[:, 0:NU, :], func=AF.Copy), [PS(bt2)], [("attnT", None)])
            op("pool", lambda e: e.tensor_tensor(out=Xp[0][:], in0=NN[:], in1=bc("bd16"), op=ALU.mult), [("NN", None)] + RCST, [("Xp0", None)])
            op("pool", lambda e: e.tensor_tensor(out=XTp[0][:], in0=NTT[:], in1=bc("bd16"), op=ALU.mult), [("NTT", None)] + RCST, [("XTp0", None)])
            op("pool", lambda e: e.tensor_tensor(out=Tn[0][:], in0=Xp[0][:], in1=bc("ident"), op=ALU.add), [("Xp0", None)] + RCST, [("T0", None)])
            op("pool", lambda e: e.tensor_tensor(out=Tt[0][:], in0=XTp[0][:], in1=bc("ident"), op=ALU.add), [("XTp0", None)] + RCST, [("Tt0", None)])
            tc = 0
            for lv in range(1, 4):
                cur = (lv - 1) % 2
                nxt = lv % 2
                bx = pair()
                bxt = pair()
                for (u, dr, i, n) in units:
                    op("pe", lambda e: e.matmul(psb[bx[dr]][:, i * 128:(i + 1) * 128], lhsT=XTp[cur][:, u, :], rhs=Xp[cur][:, u, :],
                                                start=True, stop=True), [("XTp%d" % cur, None), ("Xp%d" % cur, None)], [PS(bx[dr])])
                for (u, dr, i, n) in units:
                    op("pe", lambda e: e.matmul(psb[bxt[dr]][:, i * 128:(i + 1) * 128], lhsT=Xp[cur][:, u, :], rhs=XTp[cur][:, u, :],
                                                start=True, stop=True), [("XTp%d" % cur, None), ("Xp%d" % cur, None)], [PS(bxt[dr])])
                for dr in range(2):
                    op("act", lambda e: e.activation(out=fl(Xp[nxt], dr), in_=psb[bx[dr]][:, 0:W_], func=AF.Copy), [PS(bx[dr])], [("Xp%d" % nxt, dr)])
                    op("dve", lambda e: e.tensor_copy(out=fl(XTp[nxt], dr), in_=psb[bxt[dr]][:, 0:W_]), [PS(bxt[dr])], [("XTp%d" % nxt, dr)])
                bd = pair()
                bd2 = pair()
                tn_ = 1 - tc
                for (u, dr, i, n) in units:
                    op("pe", lambda e: e.matmul(psb[bd[dr]][:, i * 128:(i + 1) * 128], lhsT=Xp[nxt][:, u, :], rhs=Tt[tc][:, u, :],
                                                start=True, stop=True), [("Xp%d" % nxt, None), ("Tt%d" % tc, None)], [PS(bd[dr])])
                for (u, dr, i, n) in units:
                    op("pe", lambda e: e.matmul(psb[bd2[dr]][:, i * 128:(i + 1) * 128], lhsT=XTp[nxt][:, u, :], rhs=Tn[tc][:, u, :],
                                                start=True, stop=True), [("XTp%d" % nxt, None), ("T%d" % tc, None)], [PS(bd2[dr])])
                for dr in range(2):
                    op("dve", lambda e: e.tensor_tensor(out=fl(Tt[tn_], dr), in0=fl(Tt[tc], dr), in1=psb[bd[dr]][:, 0:W_], op=ALU.add),
                       [PS(bd[dr]), ("Tt%d" % tc, None)], [("Tt%d" % tn_, dr)])
                    op("dve", lambda e: e.tensor_tensor(out=fl(Tn[tn_], dr), in0=fl(Tn[tc], dr), in1=psb[bd2[dr]][:, 0:W_], op=ALU.add),
                       [PS(bd2[dr]), ("T%d" % tc, None)], [("T%d" % tn_, dr)])
                tc = tn_
            _mm = (("m32", False), ("m64", False), ("m128", True))
            if os.environ.get("V3_MERGES"):
                _mm = _mm[:int(os.environ["V3_MERGES"])]
            for (mname, last) in _mm:
                tn_ = 1 - tc
                op("pool", lambda e: e.tensor_tensor(out=EE[:], in0=NN[:], in1=bc(mname), op=ALU.mult), [("NN", None)] + RCST, [("EE", None)])
                bb1 = pair()
                for (u, dr, i, n) in units:
                    op("pe", lambda e: e.matmul(psb[bb1[dr]][:, i * 128:(i + 1) * 128], lhsT=EE[:, u, :], rhs=Tt[tc][:, u, :],
                                                start=True, stop=True), [("EE", None), ("Tt%d" % tc, None)], [PS(bb1[dr])])
                for dr in range(2):
                    op("act", lambda e: e.activation(out=fl(B1s, dr), in_=psb[bb1[dr]][:, 0:W_], func=AF.Copy), [PS(bb1[dr])], [("B1s", dr)])
                bb2 = pair()
                for (u, dr, i, n) in units:
                    op("pe", lambda e: e.matmul(psb[bb2[dr]][:, i * 128:(i + 1) * 128], lhsT=Tn[tc][:, u, :], rhs=B1s[:, u, :],
                                                start=True, stop=True), [("B1s", None), ("T%d" % tc, None)], [PS(bb2[dr])])
                for dr in range(2):
                    op("dve", lambda e: e.tensor_tensor(out=fl(Tt[tn_], dr), in0=fl(Tt[tc], dr), in1=psb[bb2[dr]][:, 0:W_], op=ALU.add),
                       [PS(bb2[dr]), ("Tt%d" % tc, None)], [("Tt%d" % tn_, dr)])
                if not last:
                    op("pool", lambda e: e.tensor_tensor(out=EET[:], in0=NTT[:], in1=bc(mname), op=ALU.mult), [("NTT", None)] + RCST, [("EET", None)])
                    ba1 = pair()
                    for (u, dr, i, n) in units:
                        op("pe", lambda e: e.matmul(psb[ba1[dr]][:, i * 128:(i + 1) * 128], lhsT=EET[:, u, :], rhs=Tn[tc][:, u, :],
                                                    start=True, stop=True), [("EET", None), ("T%d" % tc, None)], [PS(ba1[dr])])
                    for dr in range(2):
                        op("act", lambda e: e.activation(out=fl(A1s, dr), in_=psb[ba1[dr]][:, 0:W_], func=AF.Copy), [PS(ba1[dr])], [("A1s", dr)])
                    ba2 = pair()
                    for (u, dr, i, n) in units:
                        op("pe", lambda e: e.matmul(psb[ba2[dr]][:, i * 128:(i + 1) * 128], lhsT=Tt[tc][:, u, :], rhs=A1s[:, u, :],
                                                    start=True, stop=True), [("A1s", None), ("Tt%d" % tc, None)], [PS(ba2[dr])])
                    for dr in range(2):
                        op("dve", lambda e: e.tensor_tensor(out=fl(Tn[tn_], dr), in0=fl(Tn[tc], dr), in1=psb[ba2[dr]][:, 0:W_], op=ALU.add),
                           [PS(ba2[dr]), ("T%d" % tc, None)], [("T%d" % tn_, dr)])
                tc = tn_
            TF = Tt[tc]
            TFn = "Tt%d" % tc
            if os.environ.get("V3_DUMP") and gk == 0 and h == 0 and l == 0:
                for nm_, t_ in (("dbgTF", TF), ("dbgNN", NN), ("dbgB1", B1s), ("dbgT", Tn[tc])):
                    dd = nc.dram_tensor(nm_, [128, NU * 128], F32, kind="ExternalOutput").ap()
                    op("sp", lambda e: e.dma_start(out=dd, in_=t_[:].rearrange("p a b -> p (a b)")), [(TFn, None), ("NN", None), ("B1s", None), ("T0", None), ("T1", None)], [(nm_, None)], dma=True)
            for (u, dr, i, n) in units:
                op("pool", lambda e: e.tensor_scalar(out=vb[:, u, :], in0=v_tm[:, n, :], scalar1=col(beta, dr * AH + h, n),
                                                     scalar2=None, op0=ALU.mult), [("v_tm", None), ("beta", None)], [(vbn, u)])
                op("pool", lambda e: e.tensor_scalar(out=kbeg[:, u, :], in0=k_tm[:, n, :], scalar1=hs[:, 3, dr, n:n + 1],
                                                     scalar2=None, op0=ALU.mult), [("k_tm", None), ("hs", None)], [(kbegn, u)])
                op("pool", lambda e: e.tensor_scalar(out=kdec[:, u, :], in0=k_tm[:, n, :], scalar1=hs[:, 1, dr, n:n + 1],
                                                     scalar2=None, op0=ALU.mult), [("k_tm", None), ("hs", None)], [(kdecn, u)])
                op("pool", lambda e: e.tensor_copy(out=qf32[:, u, :], in_=q_fm[:, n * 128:(n + 1) * 128]),
                   [("q_fm", None)], [("qf32", u)])
            bu = pair()
            bw = pair()
            for (u, dr, i, n) in units:
                op("pe", lambda e: e.matmul(psb[bu[dr]][:, i * 128:(i + 1) * 128], lhsT=TF[:, u, :], rhs=vb[:, u, :], start=True, stop=True),
                   [(TFn, None), (vbn, None)], [PS(bu[dr])])
            for (u, dr, i, n) in units:
                op("pe", lambda e: e.matmul(psb[bw[dr]][:, i * 128:(i + 1) * 128], lhsT=kbeg[:, u, :], rhs=TF[:, u, :], start=True, stop=True),
                   [(TFn, None), (kbegn, None)], [PS(bw[dr])])
            for dr in range(2):
                op("act", lambda e: e.activation(out=fl(ubuf, dr), in_=psb[bu[dr]][:, 0:W_], func=AF.Copy), [PS(bu[dr])], [("ubuf", dr)])
                op("act", lambda e: e.activation(out=fl(wT, dr), in_=psb[bw[dr]][:, 0:W_], func=AF.Copy), [PS(bw[dr])], [(wTn, dr)])
            for i in range(NI):
                for dr in range(2):
                    u = dr * NI + i
                    k_ = gk * NI + i
                    n = k_ if dr == 0 else NCH - 1 - k_
                    sbk = 6 + dr
                    Sn = "S%d" % dr
                    if k_ == CPS:
                        op("dve", lambda e: e.tensor_scalar(out=Sst[dr][:], in0=Sst[dr][:], scalar1=linkt[:, 0:1], scalar2=None, op0=ALU.mult),
                           [(Sn, None), ("link", None)], [(Sn, None)])
                    op("pe", lambda e: e.matmul(psb[sbk][:, 0:128], lhsT=wT[:, u, :], rhs=Sst[dr][:], start=True, stop=True),
                       [(wTn, None), (Sn, None)], [PS(sbk)])
                    op("pe", lambda e: e.matmul(psb[sbk][:, 128:256], lhsT=qf32[:, u, :], rhs=Sst[dr][:], start=True, stop=True),
                       [("qf32", None), (Sn, None)], [PS(sbk)])
                    op("dve", lambda e: e.scalar_tensor_tensor(out=vnf[:, u, :], in0=psb[sbk][:, 0:128], scalar=-1.0, in1=ubuf[:, u, :],
                                                               op0=ALU.mult, op1=ALU.add),
                       [PS(sbk), ("ubuf", None)], [(vnfn, u)])
                    op("pool", lambda e: e.tensor_copy(out=vn[:, u, :], in_=vnf[:, u, :]), [(vnfn, u)], [("vn", u)])
                    op("pe", lambda e: e.matmul(psb[sbk][:, 256:384], lhsT=attnT[:, u, :], rhs=vn[:, u, :], start=True, stop=True),
                       [("attnT", None), ("vn", u)], [PS(sbk)])
                    op("pe", lambda e: e.matmul(psb[sbk][:, 384:512], lhsT=kdec[:, u, :], rhs=vnf[:, u, :], start=True, stop=True),
                       [(kdecn, None), (vnfn, u)], [PS(sbk)])
                    op("act", lambda e: e.activation(out=avs[:, dr, :], in_=psb[sbk][:, 256:384], func=AF.Copy), [PS(sbk)], [("avs", dr)])
                    op("dve", lambda e: e.scalar_tensor_tensor(out=otmp[:, dr, :], in0=psb[sbk][:, 128:256], scalar=hs[:, 0, dr, n:n + 1],
                                                               in1=avs[:, dr, :], op0=ALU.mult, op1=ALU.add),
                       [PS(sbk), ("hs", None), ("avs", dr)], [("otmp", dr)])
                    op("pool", lambda e: e.tensor_tensor(out=o_tm[:, n, :], in0=o_tm[:, n, :], in1=otmp[:, dr, :], op=ALU.add),
                       [("o_tm", n), ("otmp", dr)], [("o_tm", n)])
                    op("dve", lambda e: e.scalar_tensor_tensor(out=Sst[dr][:], in0=Sst[dr][:], scalar=hs[:, 2, dr, n:n + 1],
                                                               in1=psb[sbk][:, 384:512], op0=ALU.mult, op1=ALU.add),
                       [(Sn, None), ("hs", None), PS(sbk)], [(Sn, None)])
        tr.barrier()
        sb.pop()
        sb.push()
        zat = sb.alloc([128, NCH, 128], BF16, "zat")
        ssq4 = sb.alloc([128, NCH], F32, "ssq4")
        sqt = sb.alloc([128, 16, 128], F32, "sqt")
        ya_tm = sb.alloc([128, NCH, 128], BF16, "ya_tm")
        ya_fm = sb.alloc([128, T], BF16, "ya_fm")
        op("sp", lambda e: e.dma_start(out=zat[:], in_=zaS[:, h * 128:(h + 1) * 128].rearrange("(n p) c -> p n c", p=128)),
           [], [("zat", None)], dma=True)
        for n0 in range(0, NCH, 16):
            op("dve", lambda e: e.tensor_tensor(out=sqt[:], in0=o_tm[:, n0:n0 + 16, :], in1=o_tm[:, n0:n0 + 16, :], op=ALU.mult),
               [("o_tm", None)], [("sqt", None)])
            op("dve", lambda e: e.tensor_reduce(out=ssq4[:, n0:n0 + 16], in_=sqt[:], axis=AX, op=ALU.add),
               [("sqt", None)], [("ssq4", n0)])
        op("dve", lambda e: e.tensor_scalar(out=ssq4[:], in0=ssq4[:], scalar1=1.0 / 128, scalar2=1e-6, op0=ALU.mult, op1=ALU.add),
           [("ssq4", None)], [("ssq4", None)])
        op("act", lambda e: e.activation(out=ssq4[:], in_=ssq4[:], func=AF.Sqrt), [("ssq4", None)], [("ssq4", None)])
        op("dve", lambda e: e.reciprocal(out=ssq4[:], in_=ssq4[:]), [("ssq4", None)], [("ssq4", None)])
        op("dve", lambda e: e.tensor_tensor(out=o_tm[:], in0=o_tm[:], in1=ssq4[:].unsqueeze(2).broadcast_to([128, NCH, 128]), op=ALU.mult),
           [("o_tm", None), ("ssq4", None)], [("o_tm", None)])
        op("dve", lambda e: e.tensor_tensor(out=o_tm[:], in0=o_tm[:], in1=normw[:].unsqueeze(1).broadcast_to([128, NCH, 128]), op=ALU.mult),
           [("o_tm", None), ("normw", None)], [("o_tm", None)])
        op("dve", lambda e: e.tensor_tensor(out=ya_tm[:], in0=o_tm[:], in1=zat[:], op=ALU.mult),
           [("o_tm", None), ("zat", None)], [("ya_tm", None)])
        tbi = 0
        for n0 in range(0, NCH, 8):
            bank = tbi % 2
            tbi += 1
            pv = bf8(bank)
            for j in range(8):
                n = n0 + j
                op("pe", lambda e: e.transpose(out=pv[:, j, :], in_=ya_tm[:, n, :], identity=ident_bf[:]),
                   [("ya_tm", None), ("identbf", None)], [PS(bank)])
            op("act", lambda e: e.activation(out=ya_fm[:, n0 * 128:(n0 + 8) * 128].rearrange("p (j t) -> p j t", j=8), in_=pv[:, 0:8, :], func=AF.Copy),
               [PS(bank)], [("ya_fm", n0)])
        op("sp", lambda e: e.dma_start(out=yaT[h * 128:(h + 1) * 128, :], in_=ya_fm[:]), [("ya_fm", None)], [("yaT", h)], dma=True)
        tr.barrier()
        sb.pop()
        sb.pop()
    sb.pop()


def phase3(nc, tr, sb, cfg, l, g):
    op = tr.op
    T, TSEG, BH, GW = cfg.T, cfg.TSEG, cfg.BH, cfg.GW
    psb, PS, C, linkt = g["psb"], g["PS"], g["C"], g["linkt"]
    qbT, kbT, vbS, zbT, ybT = g["qbT"], g["kbT"], g["vbS"], g["zbT"], g["ybT"]
    groups = ((128, 1), (512, 4), (2048, 16))
    scale = 128.0 ** -0.5
    PAD = 64 * 16
    NKT = T // 128 + 1
    sb.push()
    bandb = sb.alloc([128, 256], BF16, "bandb")
    onesb = sb.alloc([128, 128], BF16, "onesb")
    vfb = sb.alloc([128, 128], BF16, "vfb")
    vlb = sb.alloc([128, 128], BF16, "vlb")
    for (t_, nm) in ((bandb, "band"), (onesb, "ones"), (vfb, "vfirst"), (vlb, "vlast")):
        op("dve", lambda e: e.tensor_copy(out=t_[:], in_=C(nm)), [("cst", None)], [("c3" + nm, None)])
    RC3 = [("c3band", None), ("c3ones", None), ("c3vfirst", None), ("c3vlast", None)]
    acc = sb.alloc([128, T], F32, "acc")
    den = sb.alloc([128, T], F32, "den")
    qs = sb.alloc([128, T], BF16, "qs")
    ks = sb.alloc([128, T + 2 * PAD], BF16, "ks")
    vsub = [sb.alloc([128, NKT, 128], BF16, "vsub%d" % i) for i in range(2)]
    NPT = 6
    pts = [sb.alloc([128, 256], BF16, "pt%d" % i) for i in range(NPT)]
    zt = sb.alloc([128, T], BF16, "zt")
    yo = sb.alloc([128, T], BF16, "yo")
    op("pool", lambda e: e.memset(ks[:, 0:PAD], 0.0), [], [("ks", "padl")])
    op("pool", lambda e: e.memset(ks[:, PAD + T:PAD + T + PAD], 0.0), [], [("ks", "padr")])
    vi = 0
    pti = 0
    sti = 0
    bi = 0
    for h in range(BH):
        op("pool", lambda e: e.memset(acc[:], 0.0), [], [("acc", None)])
        op("pool", lambda e: e.memset(den[:], 0.0), [], [("den", None)])
        for gi, (win, d) in enumerate(groups):
            row0 = (gi * BH + h) * 128
            op("sp", lambda e: e.dma_start(out=qs[:], in_=qbT[row0:row0 + 128, :]), [], [("qs", None)], dma=True)
            op("sp", lambda e: e.dma_start(out=ks[:, PAD:PAD + T], in_=kbT[row0:row0 + 128, :]), [], [("ks", "data")], dma=True)
            L = T // d
            ntile = L // 128
            assert (TSEG // d) % 128 == 0
            jb = (TSEG // d) // 128
            for r in range(d):
                vs = vsub[vi % 2]
                vn = "vsub%d" % (vi % 2)
                vi += 1
                op("pool", lambda e: e.memset(vs[0:64, 0, :], 0.0), [], [(vn, None)])
                op("pool", lambda e: e.memset(vs[64:128, ntile, :], 0.0), [(vn, None)], [(vn, None)])
                V2 = vbS[r:T:d, gi * GW + h * 128:gi * GW + (h + 1) * 128].rearrange("(j i) c -> i j c", i=128)
                op("sp", lambda e: e.dma_start(out=vs[64:128, 0:ntile, :], in_=V2[0:64]), [(vn, None)], [(vn, None)], dma=True)
                op("sp", lambda e: e.dma_start(out=vs[0:64, 1:ntile + 1, :], in_=V2[64:128]), [(vn, None)], [(vn, None)], dma=True)
                ptof = {}

                def st_stage(j):
                    nonlocal pti, sti
                    b0 = 0 if j >= 1 else 128
                    b1 = 256 if j <= ntile - 1 else 128
                    pt = pts[pti % NPT]
                    pn = "pt%d" % (pti % NPT)
                    pti += 1
                    ptof[j] = (pt, pn)
                    slot = sti % 4
                    sti += 1
                    bank = slot
                    c0 = 0
                    sn = PS(bank)
                    kst = PAD - 64 * d + r + 128 * d * j
                    kap = ks[:, kst:kst + 127 * d + 1:d]
                    qst = r + d * (128 * (j - 1) + b0)
                    qap = qs[:, qst:qst + (b1 - b0 - 1) * d + 1:d]
                    op("pe", lambda e: e.matmul(psb[bank][:, c0 + b0:c0 + b1], lhsT=kap, rhs=qap, start=True, stop=True),
                       [("ks", None), ("qs", None)], [sn])
                    op("act", lambda e: e.activation(out=pt[:, b0:b1], in_=psb[bank][:, c0 + b0:c0 + b1], func=AF.Exp, scale=scale),
                       [sn], [(pn, None)])
                    op("pool", lambda e: e.tensor_tensor(out=pt[:, b0:b1], in0=pt[:, b0:b1], in1=bandb[:, b0:b1], op=ALU.mult),
                       [(pn, None), ("c3band", None)], [(pn, None)])
                    if j == jb:
                        op("pool", lambda e: e.tensor_scalar(out=pt[0:64, 128:256], in0=pt[0:64, 128:256],
                                                             scalar1=linkt[0:64, 0:1], scalar2=None, op0=ALU.mult),
                           [(pn, None), ("link", None)], [(pn, None)])
                        op("pool", lambda e: e.tensor_scalar(out=pt[64:128, 0:128], in0=pt[64:128, 0:128],
                                                             scalar1=linkt[64:128, 0:1], scalar2=None, op0=ALU.mult),
                           [(pn, None), ("link", None)], [(pn, None)])

                def o_stage(i):
                    nonlocal bi
                    ob = 4 + (bi % 2)
                    db = 6 + (bi % 2)
                    col = (i % 4) * 128
                    pa, pan = ptof[i]
                    pb, pbn = ptof[i + 1]
                    op("pe", lambda e: e.matmul(psb[ob][:, col:col + 128], lhsT=vs[:, i, :], rhs=pa[:, 128:256], start=True, stop=False),
                       [(vn, None), (pan, None)], [PS(ob)])
                    op("pe", lambda e: e.matmul(psb[ob][:, col:col + 128], lhsT=vs[:, i + 1, :], rhs=pb[:, 0:128], start=False, stop=True),
                       [(vn, None), (pbn, None)], [PS(ob)])
                    la = vfb if i == 0 else onesb
                    lb = vlb if i + 1 == ntile else onesb
                    op("pe", lambda e: e.matmul(psb[db][:, col:col + 128], lhsT=la[:], rhs=pa[:, 128:256], start=True, stop=False),
                       RC3 + [(pan, None)], [PS(db)])
                    op("pe", lambda e: e.matmul(psb[db][:, col:col + 128], lhsT=lb[:], rhs=pb[:, 0:128], start=False, stop=True),
                       RC3 + [(pbn, None)], [PS(db)])
                    if i % 4 == 3 or i == ntile - 1:
                        nq = (i % 4 + 1) * 128
                        i0 = i - i % 4
                        a0 = r + d * 128 * i0
                        av = acc[:, a0:a0 + d * (nq - 1) + 1:d]
                        dv = den[:, a0:a0 + d * (nq - 1) + 1:d]
                        op("dve", lambda e: e.tensor_tensor(out=av, in0=av, in1=psb[ob][:, 0:nq], op=ALU.add),
                           [("acc", None), PS(ob)], [("acc", None)])
                        op("dve", lambda e: e.tensor_tensor(out=dv, in0=dv, in1=psb[db][:, 0:nq], op=ALU.add),
                           [("den", None), PS(db)], [("den", None)])
                        bi += 1

                LOOK = 2
                for j in range(min(LOOK, ntile + 1)):
                    st_stage(j)
                for i in range(ntile):
                    if i + LOOK <= ntile:
                        st_stage(i + LOOK)
                    o_stage(i)
        op("sp", lambda e: e.dma_start(out=zt[:], in_=zbT[h * 128:(h + 1) * 128, :]), [], [("zt", None)], dma=True)
        op("dve", lambda e: e.reciprocal(out=den[:], in_=den[:]), [("den", None)], [("den", None)])
        op("dve", lambda e: e.tensor_tensor(out=acc[:], in0=acc[:], in1=den[:], op=ALU.mult),
           [("acc", None), ("den", None)], [("acc", None)])
        op("dve", lambda e: e.tensor_tensor(out=yo[:], in0=acc[:], in1=zt[:], op=ALU.mult),
           [("acc", None), ("zt", None)], [("yo", None)])
        op("sp", lambda e: e.dma_start(out=ybT[h * 128:(h + 1) * 128, :], in_=yo[:]), [("yo", None)], [("ybT", h)], dma=True)
    tr.barrier()
    sb.pop()


def phase4(nc, tr, sb, cfg, l, xsrc, xdst, g):
    op = tr.op
    D, T, TSEG, KC = cfg.D, cfg.T, cfg.TSEG, cfg.KC
    AW, GW = cfg.AW, cfg.GW
    FA, FB = AW // 128, GW // 128
    psb, PS = g["psb"], g["PS"]
    modd, norm_post = g["modd"], g["norm_post"]
    w_up_a, w_up_b, w_out = g["w_up_a"], g["w_up_b"], g["w_out"]
    yaT, ybT, gT, outd = g["yaT"], g["ybT"], g["gT"], g["outd"]
    TB4 = cfg.TB
    NT = TB4 // 128
    NDB = D // 512
    for tb in range(T // TB4):
        tok0 = tb * TB4
        seg = tok0 // TSEG
        sb.push()
        mT = sb.alloc([128, KC, TB4], BF16, "mT")
        ssq = sb.alloc([128, NT, NDB], F32, "ssq")
        sb.push()
        yat = sb.alloc([128, FA, TB4], BF16, "yat")
        ybt = sb.alloc([128, FB, TB4], BF16, "ybt")
        wua = [sb.alloc([128, FA, 512], BF16, "wua%d" % i) for i in range(2)]
        wub = [sb.alloc([128, FB, 512], BF16, "wub%d" % i) for i in range(2)]
        gab = [sb.alloc([128, 2, 512], BF16, "gab%d" % i) for i in range(3)]
        tmp = [sb.alloc([128, 2, 512], F32, "utmp%d" % i) for i in range(2)]
        op("sp", lambda e: e.dma_start(out=yat[:], in_=yaT[:, tok0:tok0 + TB4].rearrange("(fc p) t -> p fc t", p=128)),
           [("yaT", None)], [("yat", None)], dma=True)
        op("sp", lambda e: e.dma_start(out=ybt[:], in_=ybT[:, tok0:tok0 + TB4].rearrange("(fc p) t -> p fc t", p=128)),
           [("ybT", None)], [("ybt", None)], dma=True)
        it = 0
        for db in range(NDB):
            wa = wua[db % 2]; wan = "wua%d" % (db % 2)
            wbb = wub[db % 2]; wbn = "wub%d" % (db % 2)
            op("pool", lambda e: e.dma_start(out=wa[:], in_=w_up_a[l, :, db * 512:(db + 1) * 512].rearrange("(fc p) n -> p fc n", p=128)),
               [], [(wan, None)], dma=True)
            op("pool", lambda e: e.dma_start(out=wbb[:], in_=w_up_b[l, :, db * 512:(db + 1) * 512].rearrange("(fc p) n -> p fc n", p=128)),
               [], [(wbn, None)], dma=True)
            for m in range(4):
                dc = db * 4 + m
                for th in range(TB4 // 512):
                    ba = (2 * it) % 8
                    bb = (2 * it + 1) % 8
                    gb_ = gab[it % 3]; gn = "gab%d" % (it % 3)
                    tm_ = tmp[it % 2]; tn = "utmp%d" % (it % 2)
                    it += 1
                    t0 = tok0 + th * 512
                    op("sp", lambda e: e.dma_start(out=gb_[:, 0, :], in_=gT[dc * 128:(dc + 1) * 128, t0:t0 + 512]),
                       [], [(gn, 0)], dma=True)
                    op("sp", lambda e: e.dma_start(out=gb_[:, 1, :], in_=gT[D + dc * 128:D + (dc + 1) * 128, t0:t0 + 512]),
                       [], [(gn, 1)], dma=True)
                    for fc in range(FA):
                        op("pe", lambda e: e.matmul(psb[ba][:], lhsT=wa[:, fc, m * 128:(m + 1) * 128],
                                                    rhs=yat[:, fc, th * 512:(th + 1) * 512],
                                                    start=(fc == 0), stop=(fc == FA - 1)),
                           [(wan, None), ("yat", None)], [PS(ba)])
                    for fc in range(FB):
                        op("pe", lambda e: e.matmul(psb[bb][:], lhsT=wbb[:, fc, m * 128:(m + 1) * 128],
                                                    rhs=ybt[:, fc, th * 512:(th + 1) * 512],
                                                    start=(fc == 0), stop=(fc == FB - 1)),
                           [(wbn, None), ("ybt", None)], [PS(bb)])
                    op("dve", lambda e: e.tensor_tensor(out=tm_[:, 0, :], in0=psb[ba][:], in1=gb_[:, 0, :], op=ALU.mult),
                       [PS(ba), (gn, 0)], [(tn, 0)])
                    op("dve", lambda e: e.tensor_tensor(out=tm_[:, 1, :], in0=psb[bb][:], in1=gb_[:, 1, :], op=ALU.mult),
                       [PS(bb), (gn, 1)], [(tn, 1)])
                    op("dve", lambda e: e.tensor_tensor(out=mT[:, dc, th * 512:(th + 1) * 512], in0=tm_[:, 0, :],
                                                        in1=tm_[:, 1, :], op=ALU.add),
                       [(tn, None)], [("mT", (dc, th))])
        tr.barrier()
        sb.pop()
        if os.environ.get("P4STOP") == "U":
            sb.pop()
            continue
        sb.push()
        wo = [sb.alloc([128, KC, 512], BF16, "wo%d" % i) for i in range(2)]
        ot = [sb.alloc([128, 512], F32, "ot%d" % i) for i in range(4)]
        junk = sb.alloc([128, 512], BF16, "junk4")
        it = 0
        for nb in range(NDB):
            w = wo[nb % 2]; wn = "wo%d" % (nb % 2)
            op("pool", lambda e: e.dma_start(out=w[:], in_=w_out[l, :, nb * 512:(nb + 1) * 512].rearrange("(kc p) n -> p kc n", p=128)),
               [], [(wn, None)], dma=True)
            for tt in range(NT):
                bank = it % 6
                o = ot[it % 4]; on = "ot%d" % (it % 4)
                it += 1
                for kc in range(KC):
                    op("pe", lambda e: e.matmul(psb[bank][:], lhsT=mT[:, kc, tt * 128:(tt + 1) * 128], rhs=w[:, kc, :],
                                                start=(kc == 0), stop=(kc == KC - 1)),
                       [(wn, None), ("mT", None)], [PS(bank)])
                op("act", lambda e: e.activation(out=o[:], in_=psb[bank][:], func=AF.Copy), [PS(bank)], [(on, None)])
                op("act", lambda e: e.activation(out=junk[:], in_=o[:], func=AF.Square, accum_out=ssq[:, tt, nb:nb + 1]),
                   [(on, None)], [("junk4", None), ("ssq", (tt, nb))])
                r0 = tok0 + tt * 128
                op("sp", lambda e: e.dma_start(out=outd[r0:r0 + 128, nb * 512:(nb + 1) * 512], in_=o[:]),
                   [(on, None)], [("outd", (r0, nb))], dma=True)
        tr.barrier()
        sb.pop()
        if os.environ.get("P4STOP") == "O":
            sb.pop()
            continue
        sb.push()
        G2 = sb.alloc([128, D], F32, "G2")
        od = [sb.alloc([128, D], F32, "od%d" % i) for i in range(2)]
        xt = [sb.alloc([128, D], F32, "xf%d" % i) for i in range(2)]
        st = sb.alloc([128, 4 * NT], F32, "st4")
        op("sp", lambda e: e.dma_start(out=G2[:], in_=modd[l, seg:seg + 1, 2 * D:3 * D].broadcast_to([128, D])),
           [], [("G2", None)], dma=True)
        op("sp", lambda e: e.dma_start(out=od[0][:], in_=norm_post[l:l + 1, :].broadcast_to([128, D])),
           [], [("od0", None)], dma=True)
        op("dve", lambda e: e.tensor_tensor(out=G2[:], in0=G2[:], in1=od[0][:], op=ALU.mult),
           [("G2", None), ("od0", None)], [("G2", None)])
        for tt in range(NT):
            o = od[tt % 2]; on = "od%d" % (tt % 2)
            x = xt[tt % 2]; xn = "xf%d" % (tt % 2)
            r0 = tok0 + tt * 128
            op("sp", lambda e: e.dma_start(out=o[:], in_=outd[r0:r0 + 128, :]), [("outd", None)], [(on, None)], dma=True)
            op("sp", lambda e: e.dma_start(out=x[:], in_=xsrc[r0:r0 + 128, :]), [], [(xn, None)], dma=True)
            s0 = st[:, 4 * tt:4 * tt + 1]
            s1 = st[:, 4 * tt + 1:4 * tt + 2]
            s2 = st[:, 4 * tt + 2:4 * tt + 3]
            s3 = st[:, 4 * tt + 3:4 * tt + 4]
            op("dve", lambda e: e.tensor_reduce(out=s0, in_=ssq[:, tt, :], axis=mybir.AxisListType.X, op=ALU.add),
               [("ssq", None)], [("st4", tt)])
            op("dve", lambda e: e.tensor_scalar(out=s1, in0=s0, scalar1=1.0 / D, scalar2=1e-6, op0=ALU.mult, op1=ALU.add),
               [("st4", tt)], [("st4", tt)])
            op("act", lambda e: e.activation(out=s2, in_=s1, func=AF.Sqrt), [("st4", tt)], [("st4", tt)])
            op("dve", lambda e: e.reciprocal(out=s3, in_=s2), [("st4", tt)], [("st4", tt)])
            op("dve", lambda e: e.scalar_tensor_tensor(out=o[:], in0=o[:], scalar=s3, in1=G2[:], op0=ALU.mult, op1=ALU.mult),
               [(on, None), ("st4", tt), ("G2", None)], [(on, None)])
            op("dve", lambda e: e.tensor_tensor(out=o[:], in0=o[:], in1=x[:], op=ALU.add),
               [(on, None), (xn, None)], [(on, None)])
            op("sp", lambda e: e.dma_start(out=xdst[r0:r0 + 128, :], in_=o[:]), [(on, None)], [("xdst", r0)], dma=True)
        tr.barrier()
        sb.pop()
        sb.pop()


def make_in_maps(cfg, inputs, seqs):
    D, KC, DEPTH, AH = cfg.D, cfg.KC, cfg.DEPTH, cfg.AH
    f = lambda a: np.ascontiguousarray(np.asarray(a, dtype=np.float32))
    conv = f(inputs["conv_a"])
    conv_t = conv.reshape(DEPTH, 5, 3 * AH, 128).transpose(0, 3, 2, 1).reshape(DEPTH, 128, 3 * AH * 5)
    shared = {
        "cst": CONST_ARR,
        "w_ada": f(inputs["w_ada"]), "b_ada": f(inputs["b_ada"]),
        "norm_pre": f(inputs["norm_pre"]), "norm_post": f(inputs["norm_post"]),
        "w_in": f(inputs["w_in"]), "conv_t": np.ascontiguousarray(conv_t),
        "a_log": f(inputs["a_log"]).reshape(DEPTH, 2 * AH), "dt_bias": f(inputs["dt_bias"]).reshape(DEPTH, 2 * AH),
        "norm_a": f(inputs["norm_a"]), "w_up_a": f(inputs["w_up_a"]), "w_up_b": f(inputs["w_up_b"]),
        "w_out": f(inputs["w_out"]),
    }
    maps = []
    for (x, c2, link, pos) in seqs:
        m = dict(shared)
        m["x"] = f(x)
        m["cT"] = np.ascontiguousarray(f(c2).reshape(2, KC, 128).transpose(2, 1, 0))
        lk = np.zeros((128, 2), np.float32)
        lk[:, 0] = link
        lk[:, 1] = (link - 1.0) * BIG
        m["link"] = lk
        rc, rs = rope_tables(pos)
        m["ropec"] = rc
        m["ropes"] = rs
        maps.append(m)
    return maps


_PROG_CACHE = {}


def kernel(x_prompt, x_sample, c_prompt, c_sample, **weights):
    cfg = Cfg()
    xp = np.asarray(x_prompt, dtype=np.float32)
    xs = np.asarray(x_sample, dtype=np.float32)
    cp = np.asarray(c_prompt, dtype=np.float32)
    cs = np.asarray(c_sample, dtype=np.float32)
    T, TSEG, D = cfg.T, cfg.TSEG, cfg.D
    seqs = []
    for b in range(2):
        seqs.append((xp[b], np.stack([cp[b], cp[b]]), 1.0, np.arange(T)))
    for b in range(2):
        seqs.append((xs[2 * b:2 * b + 2].reshape(T, D), cs[2 * b:2 * b + 2], 0.0,
                     np.concatenate([np.arange(TSEG), np.arange(TSEG)])))
    maps = make_in_maps(cfg, weights, seqs)
    nc, _ = build_program(cfg)
    res = run_bass_kernel_spmd(nc, maps, core_ids=list(range(len(maps))))
    ys = [np.asarray(r["y"], dtype=np.float32) for r in res.results]
    y_prompt = np.stack([ys[0], ys[1]])
    y_sample = np.concatenate([ys[2].reshape(2, TSEG, D), ys[3].reshape(2, TSEG, D)])
    return (y_prompt, y_sample)
```
